# Optimizing a Trainium2 kernel written in Bass

```python
import jax, jax.numpy as jnp
from jax import lax
import numpy as np

D_MODEL = 1024
BATCH = 8
SEQ = 4096
DEPTH = 1

CHUNK = 64
EPS = 1e-6

SSD_WIDTH = D_MODEL
SSD_HEAD_DIM = 64
SSD_HEADS = SSD_WIDTH // SSD_HEAD_DIM
SSD_GROUPS = 2
SSD_STATE = 128
SSD_CONV = 4
SSD_XBC = SSD_WIDTH + 2 * SSD_GROUPS * SSD_STATE

SC_WIDTH = D_MODEL
SC_GROUPS = 16
SC_CONV = 3

MIX_WIDTH = SSD_WIDTH + SC_WIDTH
IN_SIZES = (SSD_WIDTH, SSD_XBC, SSD_HEADS, SC_WIDTH, SC_WIDTH, SC_WIDTH)
IN_COLS = SSD_WIDTH + SSD_XBC + SSD_HEADS + 3 * SC_WIDTH

FFN_HIDDEN = 2816
FFN_CONV = 3

DT_MIN = 0.001
DT_MAX = 0.1

kernel_name = "hybrid_ssd_shortconv_adaln_block"


def rmsnorm(x, w):
    xf = x.astype(jnp.float32)
    y = xf * lax.rsqrt(jnp.mean(xf * xf, axis=-1, keepdims=True) + EPS)
    return (y * w.astype(jnp.float32)).astype(x.dtype)


def group_rmsnorm(x, w, groups):
    b, l, ch = x.shape
    xf = x.astype(jnp.float32).reshape(b, l, groups, ch // groups)
    y = xf * lax.rsqrt(jnp.mean(xf * xf, axis=-1, keepdims=True) + EPS)
    return (y.reshape(b, l, ch) * w.astype(jnp.float32)).astype(x.dtype)


def causal_dwconv(x, w):
    k = w.shape[0]
    return lax.conv_general_dilated(
        x, w[:, None, :].astype(x.dtype), window_strides=(1,), padding=[(k - 1, 0)],
        dimension_numbers=('NWC', 'WIO', 'NWC'), feature_group_count=x.shape[-1])


def ssd_scan(xh, dt, a_neg, bm, cm):
    out_dtype = xh.dtype
    b, seq, nh, p = xh.shape
    g, n = bm.shape[-2:]
    r = nh // g
    nc = seq // CHUNK
    xh = xh.astype(jnp.float32)
    dt = dt.astype(jnp.float32)
    xdt = (xh * dt[..., None]).reshape(b, nc, CHUNK, g, r, p)
    a = (dt * a_neg.astype(jnp.float32)).reshape(b, nc, CHUNK, g, r)
    a = jnp.moveaxis(a, 2, -1)
    bc = bm.astype(jnp.float32).reshape(b, nc, CHUNK, g, n)
    cc = cm.astype(jnp.float32).reshape(b, nc, CHUNK, g, n)
    a_cs = jnp.cumsum(a, axis=-1)
    seg = a_cs[..., :, None] - a_cs[..., None, :]
    tri = jnp.tril(jnp.ones((CHUNK, CHUNK), dtype=bool))
    decay_in = jnp.exp(jnp.where(tri, seg, -jnp.inf))
    cb = jnp.einsum('bclgn,bcsgn->bcgls', cc, bc)
    y_diag = jnp.einsum('bcgls,bcgrls,bcsgrp->bclgrp', cb, decay_in, xdt)
    decay_to_end = jnp.exp(a_cs[..., -1:] - a_cs)
    chunk_states = jnp.einsum('bcsgn,bcgrs,bcsgrp->bcgrpn', bc, decay_to_end, xdt)
    chunk_decay = jnp.exp(a_cs[..., -1])

    def step(state, inp):
        s_c, d_c = inp
        return state * d_c[..., None, None] + s_c, state

    init = jnp.zeros((b, g, r, p, n), jnp.float32)
    _, prev = lax.scan(step, init, (jnp.moveaxis(chunk_states, 1, 0), jnp.moveaxis(chunk_decay, 1, 0)))
    prev = jnp.moveaxis(prev, 0, 1)
    y_off = jnp.einsum('bclgn,bcgrpn,bcgrl->bclgrp', cc, prev, jnp.exp(a_cs))
    y = (y_diag + y_off).reshape(b, seq, nh, p)
    return y.astype(out_dtype)


def hybrid_mixer(h, w_in, ssd_conv_w, ssd_conv_b, dt_bias, a_log, d_skip, ssd_norm_w,
                 sc_conv_w, sc_norm_w, w_out):
    b, seq, _ = h.shape
    proj = jnp.einsum('bld,de->ble', h, w_in)
    idx, acc = [], 0
    for s in IN_SIZES[:-1]:
        acc += s
        idx.append(acc)
    z, xbc, dt, sc_b, sc_c, sc_h = jnp.split(proj, idx, axis=-1)

    xbc = jax.nn.silu(causal_dwconv(xbc, ssd_conv_w) + ssd_conv_b)
    xs, bm, cm = jnp.split(xbc, [SSD_WIDTH, SSD_WIDTH + SSD_GROUPS * SSD_STATE], axis=-1)
    xs = xs.reshape(b, seq, SSD_HEADS, SSD_HEAD_DIM)
    bm = bm.reshape(b, seq, SSD_GROUPS, SSD_STATE)
    cm = cm.reshape(b, seq, SSD_GROUPS, SSD_STATE)
    dt = jax.nn.softplus(dt + dt_bias)
    a_neg = -jnp.exp(a_log)
    y_ssd = ssd_scan(xs, dt, a_neg, bm, cm) + d_skip[:, None] * xs
    y_ssd = y_ssd.reshape(b, seq, SSD_WIDTH) * jax.nn.silu(z)
    y_ssd = group_rmsnorm(y_ssd, ssd_norm_w, SSD_GROUPS)

    y_sc = sc_b * causal_dwconv(sc_c * sc_h, sc_conv_w)
    y_sc = group_rmsnorm(y_sc, sc_norm_w, SC_GROUPS)

    y = jnp.concatenate([y_ssd, y_sc], axis=-1)
    return jnp.einsum('ble,ed->bld', y, w_out)


def conv_gated_mlp(h, w_up, ffn_conv_w, ffn_conv_b, w_down):
    u = jnp.einsum('bld,df->blf', h, w_up)
    u = causal_dwconv(u, ffn_conv_w) + ffn_conv_b
    g, v = jnp.split(u, 2, axis=-1)
    return jnp.einsum('blf,fd->bld', jax.nn.silu(g) * v, w_down)


def setup_inputs(seed: int = 0) -> dict:
    key = jax.random.key(seed)
    ks = jax.random.split(key, 24)
    f32 = jnp.float32
    nrm = lambda k, shape, s: jax.random.normal(k, shape, f32) * s
    x = jax.random.normal(ks[0], (BATCH, SEQ, D_MODEL), f32)
    c = jax.random.normal(ks[1], (BATCH, D_MODEL), f32)
    w_ada = nrm(ks[2], (DEPTH, D_MODEL, 6 * D_MODEL), 0.5 * D_MODEL ** -0.5)
    b_ada = nrm(ks[3], (DEPTH, 6 * D_MODEL), 0.02)
    norm1_w = 1.0 + nrm(ks[4], (DEPTH, D_MODEL), 0.02)
    w_in = nrm(ks[5], (DEPTH, D_MODEL, IN_COLS), D_MODEL ** -0.5)
    ssd_conv_w = nrm(ks[6], (DEPTH, SSD_CONV, SSD_XBC), SSD_CONV ** -0.5)
    ssd_conv_b = nrm(ks[7], (DEPTH, SSD_XBC), 0.01)
    u = jax.random.uniform(ks[8], (DEPTH, SSD_HEADS), f32)
    dt0 = jnp.exp(u * (np.log(DT_MAX) - np.log(DT_MIN)) + np.log(DT_MIN))
    dt_bias = dt0 + jnp.log(-jnp.expm1(-dt0))
    a_log = jnp.log(jax.random.uniform(ks[9], (DEPTH, SSD_HEADS), f32, 1.0, 16.0))
    d_skip = 1.0 + nrm(ks[10], (DEPTH, SSD_HEADS), 0.02)
    ssd_norm_w = 1.0 + nrm(ks[11], (DEPTH, SSD_WIDTH), 0.02)
    sc_conv_w = nrm(ks[12], (DEPTH, SC_CONV, SC_WIDTH), SC_CONV ** -0.5)
    sc_norm_w = 1.0 + nrm(ks[13], (DEPTH, SC_WIDTH), 0.02)
    w_out = nrm(ks[14], (DEPTH, MIX_WIDTH, D_MODEL), MIX_WIDTH ** -0.5)
    norm2_w = 1.0 + nrm(ks[15], (DEPTH, D_MODEL), 0.02)
    w_up = nrm(ks[16], (DEPTH, D_MODEL, 2 * FFN_HIDDEN), D_MODEL ** -0.5)
    ffn_conv_w = nrm(ks[17], (DEPTH, FFN_CONV, 2 * FFN_HIDDEN), FFN_CONV ** -0.5)
    ffn_conv_b = nrm(ks[18], (DEPTH, 2 * FFN_HIDDEN), 0.01)
    w_down = nrm(ks[19], (DEPTH, FFN_HIDDEN, D_MODEL), FFN_HIDDEN ** -0.5)
    final_norm_w = 1.0 + nrm(ks[20], (D_MODEL,), 0.02)
    return {"x": x, "c": c, "w_ada": w_ada, "b_ada": b_ada, "norm1_w": norm1_w,
            "w_in": w_in, "ssd_conv_w": ssd_conv_w, "ssd_conv_b": ssd_conv_b,
            "dt_bias": dt_bias, "a_log": a_log, "d_skip": d_skip, "ssd_norm_w": ssd_norm_w,
            "sc_conv_w": sc_conv_w, "sc_norm_w": sc_norm_w, "w_out": w_out,
            "norm2_w": norm2_w, "w_up": w_up, "ffn_conv_w": ffn_conv_w,
            "ffn_conv_b": ffn_conv_b, "w_down": w_down, "final_norm_w": final_norm_w}


def reference(x, c, w_ada, b_ada, norm1_w, w_in, ssd_conv_w, ssd_conv_b, dt_bias, a_log,
              d_skip, ssd_norm_w, sc_conv_w, sc_norm_w, w_out, norm2_w, w_up, ffn_conv_w,
              ffn_conv_b, w_down, final_norm_w):
    c_act = jax.nn.silu(c)
    for i in range(DEPTH):
        mod = jnp.einsum('bd,de->be', c_act, w_ada[i]) + b_ada[i]
        sh1, sc1, g1, sh2, sc2, g2 = [m[:, None, :] for m in jnp.split(mod, 6, axis=-1)]
        h = rmsnorm(x, norm1_w[i]) * (1.0 + sc1) + sh1
        x = x + g1 * hybrid_mixer(h, w_in[i], ssd_conv_w[i], ssd_conv_b[i], dt_bias[i],
                                  a_log[i], d_skip[i], ssd_norm_w[i], sc_conv_w[i],
                                  sc_norm_w[i], w_out[i])
        h = rmsnorm(x, norm2_w[i]) * (1.0 + sc2) + sh2
        x = x + g2 * conv_gated_mlp(h, w_up[i], ffn_conv_w[i], ffn_conv_b[i], w_down[i])
    return rmsnorm(x, final_norm_w)
```

```python
from contextlib import ExitStack

import numpy as np
import concourse.bass as bass
import concourse.mybir as mybir
from concourse.bass_utils import run_bass_kernel_spmd

F32 = mybir.dt.float32
BF16 = mybir.dt.bfloat16
AF = mybir.ActivationFunctionType
ALU = mybir.AluOpType

D = 1024
T = 512
NTB = 4
EPS = 1e-6
RING = 6
IN_COLS = 5648
FFN = 2816

PFM = {}
_o = 0
for _n, _w in [("c", 8), ("b_sh1", 8), ("b_sc1", 8), ("b_sh2", 8), ("b_sc2", 8), ("n1w", 8), ("n2w", 8),
               ("xbc_cw", 48), ("xbc_cb", 12), ("scnw", 8), ("sc_cw", 24), ("ffn_cw", 132), ("ffn_cb", 44)]:
    PFM[_n] = (_o, _w)
    _o += _w
NPFM = _o
PBC = {}
_o = 0
for _n, _w in [("dt_bias", 16), ("a_log", 16), ("d_skip", 16), ("fnw", 1024), ("ssdnw", 1024)]:
    PBC[_n] = (_o, _w)
    _o += _w
NPBC = _o


def _fm(v):
    v = np.asarray(v, np.float32)
    return np.ascontiguousarray(v.reshape(-1, 128).T)


def pack_params(inp, b):
    pfm = np.zeros((128, NPFM), np.float32)

    def put(name, arr):
        o, w = PFM[name]
        assert arr.shape == (128, w), (name, arr.shape)
        pfm[:, o:o + w] = arr

    b_ada = np.asarray(inp["b_ada"], np.float32)[0]
    put("c", _fm(inp["c"][b]))
    put("b_sh1", _fm(b_ada[0:1024]))
    put("b_sc1", _fm(b_ada[1024:2048]))
    put("b_sh2", _fm(b_ada[3072:4096]))
    put("b_sc2", _fm(b_ada[4096:5120]))
    put("n1w", _fm(inp["norm1_w"][0]))
    put("n2w", _fm(inp["norm2_w"][0]))
    cw = np.asarray(inp["ssd_conv_w"], np.float32)[0]
    put("xbc_cw", np.ascontiguousarray(cw.reshape(4, 12, 128).transpose(2, 1, 0)).reshape(128, 48))
    put("xbc_cb", _fm(inp["ssd_conv_b"][0]))
    put("scnw", _fm(inp["sc_norm_w"][0]))
    sw = np.asarray(inp["sc_conv_w"], np.float32)[0]
    put("sc_cw", np.ascontiguousarray(sw.reshape(3, 8, 128).transpose(2, 1, 0)).reshape(128, 24))
    fw = np.asarray(inp["ffn_conv_w"], np.float32)[0]
    put("ffn_cw", np.ascontiguousarray(fw.reshape(3, 44, 128).transpose(2, 1, 0)).reshape(128, 132))
    put("ffn_cb", _fm(inp["ffn_conv_b"][0]))

    pbc = np.zeros((128, NPBC), np.float32)

    def putb(name, vec):
        o, w = PBC[name]
        pbc[:, o:o + w] = np.broadcast_to(np.asarray(vec, np.float32).reshape(1, w), (128, w))

    putb("dt_bias", inp["dt_bias"][0])
    putb("a_log", inp["a_log"][0])
    putb("d_skip", inp["d_skip"][0])
    putb("fnw", inp["final_norm_w"])
    putb("ssdnw", inp["ssd_norm_w"][0])
    pbg = np.zeros((128, 2048), np.float32)
    pbg[:, 0:1024] = b_ada[2048:3072][None, :]
    pbg[:, 1024:2048] = b_ada[5120:6144][None, :]
    return pfm, pbc, pbg


class _Eng:
    def __init__(self, name, eng, sem):
        self.name, self.eng, self.sem = name, eng, sem
        self.count = 0
        self.known = {}
        self.ops = []


class Prog:
    def __init__(self, nc, es):
        self.nc, self.es = nc, es
        self.engs = {}
        for name, eng in (("pe", nc.tensor), ("act", nc.scalar), ("dve", nc.vector), ("pool", nc.gpsimd), ("sp", nc.sync)):
            sem = es.enter_context(nc.semaphore("sem_" + name))
            self.engs[name] = _Eng(name, eng, sem)
        self.last_write = {}
        self.readers = {}
        self.nsem = 0
        self.nwaits = 0

    def new_sem(self, name):
        self.nsem += 1
        return [self.es.enter_context(self.nc.semaphore(name)), 0]

    def _deps(self, reads, writes):
        deps = []
        for k in reads:
            t = self.last_write.get(k)
            if t is not None:
                deps.append(t)
        for k in writes:
            t = self.last_write.get(k)
            if t is not None:
                deps.append(t)
            deps.extend(self.readers.get(k, {}).values())
        return deps

    def op(self, engname, fn, reads=(), writes=(), dma=None):
        e = self.engs[engname]
        need = {}
        for (sem, val, implied) in self._deps(reads, writes):
            sid = id(sem)
            if e.known.get(sid, 0) >= val:
                continue
            if sid not in need or need[sid][1] < val:
                need[sid] = (sem, val, implied)
        waits = []
        for sid, (sem, val, implied) in need.items():
            if e.known.get(sid, 0) >= val:
                continue
            waits.append((sem, val))
            e.known[sid] = val
            for s2, v2 in implied.items():
                if e.known.get(s2, 0) < v2:
                    e.known[s2] = v2
        self.nwaits += len(waits)
        if dma is None:
            e.count += 1
            sem, val, inc = e.sem, e.count, 1
        else:
            dma[1] += 16
            sem, val, inc = dma[0], dma[1], 16
        implied = dict(e.known)
        tok = (sem, val, implied)
        e.ops.append((waits, fn, sem, inc))
        for k in reads:
            self.readers.setdefault(k, {})[id(sem)] = tok
        for k in writes:
            self.last_write[k] = tok
            self.readers[k] = {}
        return tok

    def emit(self, block):
        def run(e):
            def body(eng):
                for waits, fn, sem, inc in e.ops:
                    for (s, v) in waits:
                        eng.wait_ge(s, v)
                    ins = fn(eng)
                    ins.then_inc(sem, inc)
            return body
        block.tensor(run(self.engs["pe"]))
        block.scalar(run(self.engs["act"]))
        block.vector(run(self.engs["dve"]))
        block.gpsimd(run(self.engs["pool"]))
        block.sync(run(self.engs["sp"]))


def build_program(L, dbg=False):
    NT = L // T
    nc = bass.Bass("TRN2", target_bir_lowering=False)
    es = ExitStack()

    def dram(name, shape, dt, kind):
        return nc.dram_tensor(name, list(shape), dt, kind=kind).ap()

    x_d = dram("x", [L, D], F32, "ExternalInput")
    pfm_d = dram("pfm", [128, NPFM], F32, "ExternalInput")
    pbc_d = dram("pbc", [128, NPBC], F32, "ExternalInput")
    pbg_d = dram("pbg", [128, 2048], F32, "ExternalInput")
    wada_d = dram("w_ada", [D, 6 * D], F32, "ExternalInput")
    win_d = dram("w_in", [D, IN_COLS], F32, "ExternalInput")
    wout_d = dram("w_out", [2 * D, D], F32, "ExternalInput")
    wup_d = dram("w_up", [D, 2 * FFN], F32, "ExternalInput")
    wdn_d = dram("w_down", [FFN, D], F32, "ExternalInput")
    out_d = dram("out", [L, D], F32, "ExternalOutput")
    wb_in = dram("wb_in", [11, 128, 4096], BF16, "Internal")
    wb_dt = dram("wb_dt", [128, 128], BF16, "Internal")
    wb_out = dram("wb_out", [4, 128, 4096], BF16, "Internal")
    wb_up = dram("wb_up", [11, 128, 4096], BF16, "Internal")
    wb_dn = dram("wb_dn", [6, 128, 4096], BF16, "Internal")
    dbg_d = dram("dbg", [T, D], F32, "ExternalOutput") if dbg else None

    def sb(name, shape, dt):
        return es.enter_context(nc.sbuf_tensor("s_" + name, list(shape), dt))

    def psb(name):
        return es.enter_context(nc.psum_tensor("p_" + name, [128, 512], F32))

    P = Prog(nc, es)

    pfm = sb("pfm", [128, NPFM], F32)
    pbc = sb("pbc", [128, NPBC], F32)
    identf = sb("identf", [128, 128], F32)
    ident = sb("ident", [128, 128], BF16)
    Umat = sb("Umat", [128, 128], F32)
    Lmat = sb("Lmat", [128, 128], F32)
    ones = sb("ones", [128, 128], F32)
    G64 = sb("G64", [128, 128], BF16)
    cact = sb("cact", [128, 8], F32)
    cbc = sb("cbc", [128, 8, 128], F32)
    modfm = sb("modfm", [128, 32], F32)
    s1 = sb("s1", [128, 8], F32)
    s2 = sb("s2", [128, 8], F32)
    g1bc = sb("g1bc", [128, D], F32)
    g2bc = sb("g2bc", [128, D], F32)
    aneg = sb("aneg", [128, 16], F32)
    wdt = sb("wdt", [128, 8, 16], BF16)
    ring = sb("ring", [128, RING * 4096], BF16)
    xt = sb("xt", [128, NTB, D], F32)
    junk = sb("junk", [128, D], BF16)
    ssq = sb("ssq", [128, 8], F32)
    rstd = sb("rstd", [128, 8], F32)
    xn = sb("xn", [128, NTB, D], BF16)
    hT = sb("hT", [128, 8, T], BF16)
    big = sb("big", [128, 22, T], BF16)
    yT = sb("yT", [128, 16, T], BF16)
    halo_xbc = sb("halo_xbc", [128, 12, 3], F32)
    halo_sc = sb("halo_sc", [128, 8, 2], F32)
    halo_ffn = sb("halo_ffn", [128, 2, 44, 2], F32)
    corr = sb("corr", [128, 44, 2], F32)
    tmpc = sb("tmpc", [128, 44], F32)
    cwork = sb("cwork", [128, 2, 516], F32)
    cacc = sb("cacc", [128, 4, T], F32)
    csb = sb("csb", [128, T], F32)
    sqb = sb("sqb", [128, T], BF16)
    rtb = sb("rtb", [128, T], F32)
    sgb = sb("sgb", [128, 2, T], BF16)
    tmpb = sb("tmpb", [128, 2, T], F32)
    dtt = sb("dtt", [128, NTB, 16], F32)
    dte = sb("dte", [128, NTB, 16], F32)
    av = sb("av", [128, NTB, 16], F32)
    acs = sb("acs", [128, NTB, 32], F32)
    eacs = sb("eacs", [128, NTB, 16], F32)
    darg = sb("darg", [128, NTB, 16], F32)
    wdte = sb("wdte", [128, NTB, 16], F32)
    cdec = sb("cdec", [128, NTB, 16], F32)
    xstok = sb("xstok", [128, D], BF16)
    xdt = sb("xdt", [128, D], BF16)
    xdtd = sb("xdtd", [128, D], BF16)
    btsb = sb("btsb", [128, 256], BF16)
    cbtm = sb("cbtm", [128, 2, 128], F32)
    aU = sb("aU", [128, 16, 128], F32)
    expseg = sb("expseg", [128, 8, 128], F32)
    MT = sb("MT", [128, 8, 128], BF16)
    state = sb("state", [128, D], F32)
    statebf = sb("statebf", [128, D], BF16)
    ytok = sb("ytok", [128, D], F32)
    ssqg = sb("ssqg", [128, 2], F32)
    rsg = sb("rsg", [128, 2], F32)
    yn = sb("yn", [128, D], BF16)

    zs = big[:, 0:8, :]
    xs_f = big[:, 8:16, :]
    B_f = big[:, 16:18, :]
    C_f = big[:, 18:20, :]

    pA = psb("pA")
    pB = psb("pB")
    pC = psb("pC")
    pD = psb("pD")
    pE = psb("pE")
    pF = psb("pF")
    pG = psb("pG")
    pH = psb("pH")
    acc_banks = [(pE, "pE"), (pF, "pF"), (pG, "pG"), (pC, "pC"), (pD, "pD")]
    acc_rr = [0]

    def next_acc():
        b = acc_banks[acc_rr[0] % 5]
        acc_rr[0] += 1
        return b

    pA_bf = pA[:].bitcast(BF16)
    pH_bf = pH[:, 0:256].bitcast(BF16)
    p_dtraw = pH[:, 256:320]
    p_cbt = pB[:, 0:256]
    p_bt = pB[:, 256:384].bitcast(BF16)
    p_acs = pB[:, 384:512]
    p_mod = pB[:, 0:32]

    def pf(name, j=None, w=None):
        o, ww = PFM[name]
        if j is None:
            return pfm[:, o:o + ww]
        return pfm[:, o + j:o + j + (w or 1)]

    def pb(name):
        o, ww = PBC[name]
        return pbc[:, o:o + ww]

    ld_sem = P.new_sem("ld_par")
    P.op("sp", lambda e: e.dma_start(out=pfm[:], in_=pfm_d), writes=["pfm"], dma=ld_sem)
    P.op("sp", lambda e: e.dma_start(out=pbc[:], in_=pbc_d), writes=["pbc"], dma=ld_sem)
    pbg = xn[:].rearrange("p a d -> p (a d)").bitcast(F32)
    P.op("sp", lambda e: e.dma_start(out=pbg, in_=pbg_d), writes=["xn%d" % tb for tb in range(NTB)], dma=ld_sem)

    conv_tok = {}

    def conv(key, dmas):
        sem = P.new_sem("cv_" + key)
        for (o, i) in dmas:
            P.op("pool", (lambda o, i: lambda e: e.dma_start(out=o, in_=i))(o, i), writes=["D" + key], dma=sem)

    def v3(ap2d, k, n):
        return ap2d.rearrange("p (k n) -> p k n", k=k)

    def src_cols(w, c0, n):
        return w[:, c0:c0 + n].rearrange("(k p) n -> p k n", p=128)

    sc_blocks = []
    for j in range(8):
        sc_blocks += [2576 + j * 128, 3600 + j * 128, 4624 + j * 128]
    for c in range(5):
        conv("in%d" % c, [(v3(wb_in[c], 8, 512), src_cols(win_d, c * 512, 512))])
    conv("dt", [(wb_dt.rearrange("p (k n) -> p k n", k=8), src_cols(win_d, 2560, 16))])
    for c in range(6):
        dm = []
        for q in range(4):
            dm.append((v3(wb_in[5 + c], 8, 512)[:, :, q * 128:(q + 1) * 128], src_cols(win_d, sc_blocks[c * 4 + q], 128)))
        conv("in%d" % (5 + c), dm)
    for q in range(4):
        conv("out%d" % q, [(v3(wb_out[q], 4, 1024), wout_d[q * 512:(q + 1) * 512, :].rearrange("(r p) d -> p r d", p=128))])
    for c in range(11):
        conv("up%d" % c, [
            (v3(wb_up[c], 8, 512)[:, :, 0:256], src_cols(wup_d, c * 256, 256)),
            (v3(wb_up[c], 8, 512)[:, :, 256:512], src_cols(wup_d, FFN + c * 256, 256)),
        ])
    for q in range(6):
        nr = 4 if q < 5 else 2
        conv("dn%d" % q, [(v3(wb_dn[q], 4, 1024)[:, 0:nr, :], wdn_d[q * 512:q * 512 + nr * 128, :].rearrange("(r p) d -> p r d", p=128))])

    P.op("pool", lambda e: e.memset(identf[:], 0.0), writes=["identf"])
    P.op("pool", lambda e: e.affine_select(out=identf[:], in_=identf[:], pattern=[[-1, 128]], compare_op=ALU.not_equal,
                                           fill=1.0, base=0, channel_multiplier=1), reads=["identf"], writes=["identf"])
    P.op("pool", lambda e: e.tensor_copy(out=ident[:], in_=identf[:]), reads=["identf"], writes=["ident"])
    P.op("pool", lambda e: e.memset(Umat[:], 1.0), writes=["Umat"])
    P.op("pool", lambda e: e.affine_select(out=Umat[:], in_=Umat[:], pattern=[[1, 128]], compare_op=ALU.is_ge,
                                           fill=0.0, base=0, channel_multiplier=-1), reads=["Umat"], writes=["Umat"])
    P.op("pool", lambda e: e.memset(Lmat[:], 1.0), writes=["Lmat"])
    P.op("pool", lambda e: e.affine_select(out=Lmat[:], in_=Lmat[:], pattern=[[-1, 128]], compare_op=ALU.is_gt,
                                           fill=0.0, base=0, channel_multiplier=1), reads=["Lmat"], writes=["Lmat"])
    P.op("pool", lambda e: e.memset(ones[:], 1.0), writes=["ones"])
    P.op("dve", lambda e: e.memset(G64[:], 0.0), writes=["G64"])
    P.op("dve", lambda e: e.memset(G64[0:64, 0:64], 1.0 / 64), reads=["G64"], writes=["G64"])
    P.op("dve", lambda e: e.memset(G64[64:128, 64:128], 1.0 / 64), reads=["G64"], writes=["G64"])
    P.op("dve", lambda e: e.memset(state[:], 0.0), writes=["state0", "state1"])
    P.op("dve", lambda e: e.memset(statebf[:], 0.0), writes=["statebf0", "statebf1"])
    P.op("pool", lambda e: e.memset(halo_xbc[:], 0.0), writes=["halo_xbc%d" % j for j in range(12)])
    P.op("pool", lambda e: e.memset(halo_sc[:], 0.0), writes=["halo_sc%d" % j for j in range(8)])
    P.op("pool", lambda e: e.memset(halo_ffn[:], 0.0), writes=["halo_ffn_p0", "halo_ffn_p1"])

    P.op("act", lambda e: e.activation(out=cact[:], in_=pf("c"), func=AF.Silu), reads=["pfm"], writes=["cact"])
    P.op("dve", lambda e: e.tensor_copy(out=cbc[:], in_=cact[:].unsqueeze(2).to_broadcast([128, 8, 128])), reads=["cact"], writes=["cbc"])
    P.op("act", lambda e: e.activation(out=aneg[:], in_=pb("a_log"), func=AF.Exp), reads=["pbc"], writes=["aneg"])
    P.op("dve", lambda e: e.tensor_scalar(out=aneg[:], in0=aneg[:], scalar1=-1.0, scalar2=None, op0=ALU.mult), reads=["aneg"], writes=["aneg"])
    P.op("sp", lambda e: e.dma_start(out=wdt[:], in_=wb_dt.rearrange("p (k n) -> p k n", k=8)), reads=["Ddt"], writes=["wdt"], dma=ld_sem)

    stage = ring[:, :].bitcast(F32)
    st_sems = [P.new_sem("st%d" % i) for i in range(3)]
    fm_slot = {0: 0, 1: 1, 3: 2, 4: 3}
    hg = 0
    for g in range(6):
        for half in range(2):
            s = hg % 3
            sv = stage[:, s * 4096:(s + 1) * 4096].rearrange("p (k n) -> p k n", k=8)
            skey = "stage%d" % s
            c0 = g * 1024 + half * 512
            P.op("sp", (lambda sv, c0: lambda e: e.dma_start(out=sv, in_=src_cols(wada_d, c0, 512)))(sv, c0),
                 writes=[skey], dma=st_sems[s])
            if g in fm_slot:
                def mm(e, sv=sv, g=g, half=half):
                    ins = None
                    for blk in range(4):
                        col = fm_slot[g] * 8 + half * 4 + blk
                        for k in range(8):
                            ins = e.matmul(p_mod[:, col:col + 1], lhsT=sv[:, k, blk * 128:(blk + 1) * 128], rhs=cact[:, k:k + 1],
                                           start=(k == 0), stop=(k == 7))
                    return ins
                P.op("pe", mm, reads=[skey, "cact"], writes=["pBmod%d_%d" % (g, half)])
            else:
                bank, bkey = next_acc()

                def mm(e, sv=sv, bank=bank):
                    ins = None
                    for k in range(8):
                        ins = e.matmul(bank[:], lhsT=cbc[:, k, :], rhs=sv[:, k, :], start=(k == 0), stop=(k == 7))
                    return ins
                P.op("pe", mm, reads=[skey, "cbc"], writes=[bkey])
                gb = g1bc if g == 2 else g2bc
                o = 0 if g == 2 else 1024
                P.op("dve", (lambda gb, bank, o, half: lambda e: e.tensor_tensor(
                    out=gb[:, half * 512:(half + 1) * 512], in0=bank[:], in1=pbg[:, o + half * 512:o + (half + 1) * 512], op=ALU.add))(gb, bank, o, half),
                    reads=[bkey] + ["xn%d" % tb for tb in range(NTB)], writes=["g1bc" if g == 2 else "g2bc"])
            hg += 1
    modkeys = ["pBmod%d_%d" % (g, h) for g in fm_slot for h in range(2)]
    o_b = PFM["b_sh1"][0]
    P.op("dve", lambda e: e.tensor_tensor(out=modfm[:], in0=p_mod, in1=pfm[:, o_b:o_b + 32], op=ALU.add), reads=modkeys + ["pfm"], writes=["modfm"])
    P.op("dve", lambda e: e.scalar_tensor_tensor(out=s1[:], in0=modfm[:, 8:16], scalar=1.0, in1=pf("n1w"), op0=ALU.add, op1=ALU.mult),
         reads=["modfm", "pfm"], writes=["s1"])
    P.op("dve", lambda e: e.scalar_tensor_tensor(out=s2[:], in0=modfm[:, 24:32], scalar=1.0, in1=pf("n2w"), op0=ALU.add, op1=ALU.mult),
         reads=["modfm", "pfm"], writes=["s2"])
    sh1 = modfm[:, 0:8]
    sh2 = modfm[:, 16:24]
    ring_keys = ["stage0", "stage1", "stage2"]

    seq = [("in", c) for c in range(11)] + [("out", q) for q in range(4)] + [("up", c) for c in range(11)] + [("dn", q) for q in range(6)]
    NCH = len(seq)
    slot_sems = [P.new_sem("slot%d" % i) for i in range(RING)]
    loaded = [0]

    def chunk_src(kind, idx):
        return {"in": wb_in, "out": wb_out, "up": wb_up, "dn": wb_dn}[kind][idx]

    def ensure_loaded(upto):
        upto = min(upto, NT * NCH - 1)
        while loaded[0] <= upto:
            m = loaded[0]
            kind, idx = seq[m % NCH]
            s = m % RING
            extra = ring_keys if m < RING else []
            src = chunk_src(kind, idx)
            if kind == "dn" and idx == 5:
                P.op("sp", (lambda s, src: lambda e: e.dma_start(out=ring[:, s * 4096:s * 4096 + 2048], in_=src[:, 0:2048]))(s, src),
                     reads=["D%s%d" % (kind, idx)], writes=["slot%d" % s] + extra, dma=slot_sems[s])
            else:
                P.op("sp", (lambda s, src: lambda e: e.dma_start(out=ring[:, s * 4096:(s + 1) * 4096], in_=src))(s, src),
                     reads=["D%s%d" % (kind, idx)], writes=["slot%d" % s] + extra, dma=slot_sems[s])
            loaded[0] += 1

    LA = RING - 1

    def wslot(tile, n):
        m = tile * NCH + n
        ensure_loaded(m + LA if seq[n][0] in ("in", "up") else m)
        s = m % RING
        return ring[:, s * 4096:(s + 1) * 4096], "slot%d" % s

    x_sem = P.new_sem("xld")
    o_sem = P.new_sem("ost")

    def rmsnorm_to_hT(scale_t, shift_t, skey):
        for tb in range(NTB):
            P.op("dve", (lambda tb: lambda e: e.scalar_tensor_tensor(out=junk[:], in0=xt[:, tb, :], scalar=1.0, in1=xt[:, tb, :],
                                                                      op0=ALU.mult, op1=ALU.mult, accum_out=ssq[:, tb:tb + 1]))(tb),
                 reads=["xt%d" % tb], writes=["junk", "ssq"])
        P.op("act", lambda e: e.activation(out=rstd[:, 0:4], in_=ssq[:, 0:4], func=AF.Sqrt, bias=EPS, scale=1.0 / D), reads=["ssq"], writes=["rstd"])
        P.op("dve", lambda e: e.reciprocal(out=rstd[:, 0:4], in_=rstd[:, 0:4]), reads=["rstd"], writes=["rstd"])
        for tb in range(NTB):
            P.op("act", (lambda tb: lambda e: e.activation(out=xn[:, tb, :], in_=xt[:, tb, :], func=AF.Identity, scale=rstd[:, tb:tb + 1]))(tb),
                 reads=["xt%d" % tb, "rstd"], writes=["xn%d" % tb])
        for fb in range(8):
            def tr(e, fb=fb):
                ins = None
                for tb in range(NTB):
                    ins = e.transpose(out=pH_bf[:, tb * 128:(tb + 1) * 128], in_=xn[:, tb, fb * 128:(fb + 1) * 128], identity=ident[:])
                return ins
            P.op("pe", tr, reads=["xn%d" % tb for tb in range(NTB)] + ["ident"], writes=["pHt"])
            P.op("act", (lambda fb: lambda e: e.activation(out=hT[:, fb, :], in_=pH_bf, func=AF.Identity,
                                                           bias=shift_t[:, fb:fb + 1], scale=scale_t[:, fb:fb + 1]))(fb),
                 reads=["pHt", skey, "modfm"], writes=["hT%d" % fb])

    hT_keys = ["hT%d" % fb for fb in range(8)]
    dsem = P.new_sem("dbgst") if dbg else None
    xt_keys = ["xt%d" % tb for tb in range(NTB)]

    for ti in range(NT):
        P.op("sp", (lambda ti: lambda e: e.dma_start(out=xt[:], in_=x_d[ti * T:(ti + 1) * T, :].rearrange("(tb p) d -> p tb d", p=128)))(ti),
             writes=xt_keys, dma=x_sem)
        ensure_loaded(ti * NCH + LA)
        rmsnorm_to_hT(s1, sh1, "s1")

        def mm_dt(e):
            ins = None
            for tb in range(NTB):
                for k in range(8):
                    ins = e.matmul(p_dtraw[:, tb * 16:(tb + 1) * 16], lhsT=hT[:, k, tb * 128:(tb + 1) * 128], rhs=wdt[:, k, :],
                                   start=(k == 0), stop=(k == 7))
            return ins
        P.op("pe", mm_dt, reads=hT_keys + ["wdt"], writes=["pHdt"])
        dtr3 = p_dtraw.rearrange("p (t h) -> p t h", t=NTB)
        P.op("dve", lambda e: e.tensor_tensor(out=dtt[:], in0=dtr3, in1=pb("dt_bias").unsqueeze(1).to_broadcast([128, NTB, 16]), op=ALU.add),
             reads=["pHdt", "pbc"], writes=["dtt"])
        P.op("dve", lambda e: e.tensor_scalar(out=dtt[:], in0=dtt[:], scalar1=20.0, scalar2=None, op0=ALU.min), reads=["dtt"], writes=["dtt"])
        P.op("act", lambda e: e.activation(out=dtt[:], in_=dtt[:], func=AF.Exp), reads=["dtt"], writes=["dtt"])
        P.op("act", lambda e: e.activation(out=dtt[:], in_=dtt[:], func=AF.Ln, bias=1.0, scale=1.0), reads=["dtt"], writes=["dtt"])
        P.op("dve", lambda e: e.tensor_tensor(out=av[:], in0=dtt[:], in1=aneg[:].unsqueeze(1).to_broadcast([128, NTB, 16]), op=ALU.mult),
             reads=["dtt", "aneg"], writes=["av"])

        def mm_acs(e):
            ins = None
            for tb in range(NTB):
                e.matmul(p_acs[:, tb * 32:tb * 32 + 16], lhsT=Umat[:], rhs=av[:, tb, :], start=True, stop=True)
                ins = e.matmul(p_acs[:, tb * 32 + 16:tb * 32 + 32], lhsT=ones[:], rhs=av[:, tb, :], start=True, stop=True)
            return ins
        P.op("pe", mm_acs, reads=["av", "Umat", "ones"], writes=["pBacs"])
        P.op("act", lambda e: e.activation(out=acs[:], in_=p_acs.rearrange("p (t h) -> p t h", t=NTB), func=AF.Identity), reads=["pBacs"], writes=["acs"])
        P.op("act", lambda e: e.activation(out=eacs[:], in_=acs[:, :, 0:16], func=AF.Exp), reads=["acs"], writes=["eacs"])
        P.op("act", lambda e: e.activation(out=cdec[:], in_=acs[:, :, 16:32], func=AF.Exp), reads=["acs"], writes=["cdec"])
        P.op("dve", lambda e: e.tensor_tensor(out=darg[:], in0=acs[:, :, 16:32], in1=acs[:, :, 0:16], op=ALU.subtract), reads=["acs"], writes=["darg"])
        P.op("act", lambda e: e.activation(out=dte[:], in_=darg[:], func=AF.Exp), reads=["darg"], writes=["dte"])
        P.op("dve", lambda e: e.tensor_tensor(out=wdte[:], in0=dte[:], in1=dtt[:], op=ALU.mult), reads=["dte", "dtt"], writes=["wdte"])

        zs_v = zs.rearrange("p a t -> p (a t)")
        for half in range(2):
            for tb in range(NTB):
                ws, wkey = wslot(ti, half)
                w3 = ws.rearrange("p (k n) -> p k n", k=8)
                bank, bkey = next_acc()

                def mm(e, w3=w3, bank=bank, tb=tb):
                    ins = None
                    for k in range(8):
                        ins = e.matmul(bank[:], lhsT=hT[:, k, tb * 128:(tb + 1) * 128], rhs=w3[:, k, :], start=(k == 0), stop=(k == 7))
                    return ins
                P.op("pe", mm, reads=hT_keys + [wkey], writes=[bkey])
                P.op("act", (lambda bank, tb, half: lambda e: e.activation(out=zs_v[:, tb * 1024 + half * 512:tb * 1024 + (half + 1) * 512],
                                                                             in_=bank[:], func=AF.Silu))(bank, tb, half),
                     reads=[bkey], writes=["zs%d" % tb, "big"])

        for j in range(12):
            n = 2 + j // 4
            ws, wkey = wslot(ti, n)
            w3 = ws.rearrange("p (k n) -> p k n", k=8)
            bank, bkey = next_acc()

            def mm(e, w3=w3, bank=bank, q=j % 4):
                ins = None
                for k in range(8):
                    ins = e.matmul(bank[:], lhsT=w3[:, k, q * 128:(q + 1) * 128], rhs=hT[:, k, :], start=(k == 0), stop=(k == 7))
                return ins
            P.op("pe", mm, reads=hT_keys + [wkey], writes=[bkey])
            wb_ = cwork[:, j % 2, :]
            wk = "cwork%d" % (j % 2)
            ak = "cacc%d" % (j % 2)
            acc = cacc[:, j % 2, :]
            hk = "halo_xbc%d" % j
            cw = lambda k, j=j: pfm[:, PFM["xbc_cw"][0] + j * 4 + k:PFM["xbc_cw"][0] + j * 4 + k + 1]
            cb = pfm[:, PFM["xbc_cb"][0] + j:PFM["xbc_cb"][0] + j + 1]
            P.op("act", (lambda wb_, bank: lambda e: e.activation(out=wb_[:, 3:515], in_=bank[:], func=AF.Identity))(wb_, bank),
                 reads=[bkey], writes=[wk])
            P.op("pool", (lambda wb_, j: lambda e: e.tensor_copy(out=wb_[:, 0:3], in_=halo_xbc[:, j, :]))(wb_, j), reads=[hk], writes=[wk])
            P.op("pool", (lambda wb_, j: lambda e: e.tensor_copy(out=halo_xbc[:, j, :], in_=wb_[:, 512:515]))(wb_, j), reads=[wk], writes=[hk])
            P.op("act", (lambda acc, bank, cw, cb: lambda e: e.activation(out=acc, in_=bank[:], func=AF.Identity, bias=cb, scale=cw(3)))(acc, bank, cw, cb),
                 reads=[bkey, "pfm"], writes=[ak])
            for k in range(3):
                P.op("dve", (lambda acc, wb_, cw, k: lambda e: e.scalar_tensor_tensor(out=acc, in0=wb_[:, k:k + 512], scalar=cw(k), in1=acc,
                                                                                      op0=ALU.mult, op1=ALU.add))(acc, wb_, cw, k),
                     reads=[wk, ak, "pfm"], writes=[ak])
            if j < 8:
                dst, dk = xs_f[:, j, :], "xs_f%d" % j
            elif j < 10:
                dst, dk = B_f[:, j - 8, :], "B_f%d" % (j - 8)
            else:
                dst, dk = C_f[:, j - 10, :], "C_f%d" % (j - 10)
            P.op("act", (lambda dst, acc: lambda e: e.activation(out=dst, in_=acc, func=AF.Silu))(dst, acc), reads=[ak], writes=[dk, "big"])

        for c in range(NTB):
            tsl = slice(c * 128, (c + 1) * 128)

            def tr_xs(e, tsl=tsl):
                ins = None
                for fb in range(8):
                    ins = e.transpose(out=pA_bf[:, fb * 128:(fb + 1) * 128], in_=xs_f[:, fb, tsl], identity=ident[:])
                return ins
            P.op("pe", tr_xs, reads=["xs_f%d" % j for j in range(8)] + ["ident"], writes=["pA"])

            def tr_b(e, tsl=tsl):
                ins = None
                for g in range(2):
                    ins = e.transpose(out=p_bt[:, g * 128:(g + 1) * 128], in_=B_f[:, g, tsl], identity=ident[:])
                return ins
            P.op("pe", tr_b, reads=["B_f0", "B_f1", "ident"], writes=["pBbt"])

            def mm_cbt(e, tsl=tsl):
                ins = None
                for g in range(2):
                    ins = e.matmul(p_cbt[:, g * 128:(g + 1) * 128], lhsT=B_f[:, g, tsl], rhs=C_f[:, g, tsl], start=True, stop=True)
                return ins
            P.op("pe", mm_cbt, reads=["B_f0", "B_f1", "C_f0", "C_f1"], writes=["pBcbt"])
            P.op("act", lambda e: e.activation(out=xstok[:], in_=pA_bf, func=AF.Identity), reads=["pA"], writes=["xstok"])
            P.op("act", lambda e: e.activation(out=btsb[:], in_=p_bt, func=AF.Identity), reads=["pBbt"], writes=["btsb"])
            xs3 = xstok[:].rearrange("p (h q) -> p h q", h=16)
            P.op("dve", (lambda c: lambda e: e.tensor_tensor(out=xdt[:].rearrange("p (h q) -> p h q", h=16), in0=xs3,
                                                              in1=dtt[:, c, :].unsqueeze(2).to_broadcast([128, 16, 64]), op=ALU.mult))(c),
                 reads=["xstok", "dtt"], writes=["xdt"])
            P.op("pool", (lambda c: lambda e: e.tensor_tensor(out=xdtd[:].rearrange("p (h q) -> p h q", h=16), in0=xs3,
                                                               in1=wdte[:, c, :].unsqueeze(2).to_broadcast([128, 16, 64]), op=ALU.mult))(c),
                 reads=["xstok", "wdte"], writes=["xdtd"])
            P.op("dve", lambda e: e.tensor_tensor(out=cbtm[:], in0=p_cbt.rearrange("p (g l) -> p g l", g=2),
                                                  in1=Umat[:].unsqueeze(1).to_broadcast([128, 2, 128]), op=ALU.mult),
                 reads=["pBcbt", "Umat"], writes=["cbtm"])
            P.op("dve", (lambda c: lambda e: e.tensor_tensor(out=aU[:], in0=Umat[:].unsqueeze(1).to_broadcast([128, 16, 128]),
                                                               in1=av[:, c, :].unsqueeze(2).to_broadcast([128, 16, 128]), op=ALU.mult))(c),
                 reads=["av", "Umat"], writes=["aU"])
            for g in range(2):
                def mm_seg(e, g=g):
                    e.matmul(pC[:], lhsT=Lmat[:], rhs=aU[:, g * 8:g * 8 + 4, :].rearrange("p h l -> p (h l)"), start=True, stop=True)
                    return e.matmul(pD[:], lhsT=Lmat[:], rhs=aU[:, g * 8 + 4:g * 8 + 8, :].rearrange("p h l -> p (h l)"), start=True, stop=True)
                P.op("pe", mm_seg, reads=["aU", "Lmat"], writes=["pC", "pD"])
                P.op("act", lambda e: e.activation(out=expseg[:, 0:4, :].rearrange("p h l -> p (h l)"), in_=pC[:], func=AF.Exp), reads=["pC"], writes=["expseg"])
                P.op("act", lambda e: e.activation(out=expseg[:, 4:8, :].rearrange("p h l -> p (h l)"), in_=pD[:], func=AF.Exp), reads=["pD"], writes=["expseg"])
                P.op("dve", (lambda g: lambda e: e.tensor_tensor(out=MT[:], in0=expseg[:], in1=cbtm[:, g, :].unsqueeze(1).to_broadcast([128, 8, 128]),
                                                                  op=ALU.mult))(g), reads=["expseg", "cbtm"], writes=["MT"])
                (byd, kyd), (byo, kyo), (bcs, kcs) = (pE, "pE"), (pF, "pF"), (pG, "pG")

                def mm_y(e, g=g, tsl=tsl):
                    for h in range(8):
                        e.matmul(byd[:, h * 64:(h + 1) * 64], lhsT=MT[:, h, :], rhs=xdt[:, (g * 8 + h) * 64:(g * 8 + h + 1) * 64], start=True, stop=True)
                    e.matmul(byo[:], lhsT=C_f[:, g, tsl], rhs=statebf[:, g * 512:(g + 1) * 512], start=True, stop=True)
                    return e.matmul(bcs[:], lhsT=btsb[:, g * 128:(g + 1) * 128], rhs=xdtd[:, g * 512:(g + 1) * 512], start=True, stop=True)
                P.op("pe", mm_y, reads=["MT", "xdt", "C_f%d" % g, "statebf%d" % g, "btsb", "xdtd"], writes=[kyd, kyo, kcs])
                gs = slice(g * 512, (g + 1) * 512)
                yv = ytok[:, gs].rearrange("p (h q) -> p h q", h=8)
                P.op("dve", (lambda c, g, yv: lambda e: e.tensor_tensor(out=yv, in0=byo[:].rearrange("p (h q) -> p h q", h=8),
                                                                         in1=eacs[:, c, g * 8:(g + 1) * 8].unsqueeze(2).to_broadcast([128, 8, 64]), op=ALU.mult))(c, g, yv),
                     reads=[kyo, "eacs"], writes=["ytok%d" % g])
                P.op("dve", (lambda gs: lambda e: e.tensor_tensor(out=ytok[:, gs], in0=ytok[:, gs], in1=byd[:], op=ALU.add))(gs),
                     reads=[kyd, "ytok%d" % g], writes=["ytok%d" % g])
                P.op("pool", (lambda g, gs: lambda e: e.tensor_tensor(out=rtb[:].rearrange("p (h q) -> p h q", h=8),
                                                                      in0=xstok[:, gs].rearrange("p (h q) -> p h q", h=8),
                                                                      in1=pb("d_skip")[:, g * 8:(g + 1) * 8].unsqueeze(2).to_broadcast([128, 8, 64]), op=ALU.mult))(g, gs),
                     reads=["xstok", "pbc"], writes=["rtb"])
                P.op("pool", (lambda gs: lambda e: e.tensor_tensor(out=ytok[:, gs], in0=ytok[:, gs], in1=rtb[:], op=ALU.add))(gs),
                     reads=["rtb", "ytok%d" % g], writes=["ytok%d" % g])
                P.op("dve", (lambda c, gs: lambda e: e.tensor_tensor(out=ytok[:, gs], in0=ytok[:, gs], in1=zs_v[:, c * 1024 + gs.start:c * 1024 + gs.stop], op=ALU.mult))(c, gs),
                     reads=["zs%d" % c, "ytok%d" % g], writes=["ytok%d" % g])
                P.op("dve", (lambda g, gs: lambda e: e.scalar_tensor_tensor(out=junk[:, 0:512], in0=ytok[:, gs], scalar=1.0, in1=ytok[:, gs], op0=ALU.mult, op1=ALU.mult,
                                                                             accum_out=ssqg[:, g:g + 1]))(g, gs),
                     reads=["ytok%d" % g], writes=["junk", "ssqg"])
                sv_ = state[:, gs].rearrange("p (h q) -> p h q", h=8)
                P.op("dve", (lambda c, g, sv_: lambda e: e.tensor_tensor(out=sv_, in0=sv_, in1=cdec[:, c, g * 8:(g + 1) * 8].unsqueeze(2).to_broadcast([128, 8, 64]), op=ALU.mult))(c, g, sv_),
                     reads=["cdec", "state%d" % g], writes=["state%d" % g])
                P.op("dve", (lambda gs: lambda e: e.tensor_tensor(out=state[:, gs], in0=state[:, gs], in1=bcs[:], op=ALU.add))(gs),
                     reads=[kcs, "state%d" % g], writes=["state%d" % g])
                P.op("act", (lambda gs: lambda e: e.activation(out=statebf[:, gs], in_=state[:, gs], func=AF.Identity))(gs),
                     reads=["state%d" % g], writes=["statebf%d" % g])
            if dbg and ti == 0 and c in (0, 1):
                P.op("sp", (lambda c: lambda e: e.dma_start(out=dbg_d[c * 128:(c + 1) * 128, :], in_=ytok[:]))(c), reads=["ytok0", "ytok1"], dma=dsem)
            if dbg and ti == 0 and c == 3:
                P.op("sp", lambda e: e.dma_start(out=dbg_d[256:384, :], in_=state[:]), reads=["state0", "state1"], dma=dsem)
                P.op("pool", lambda e: e.dma_start(out=dbg_d[384:512, :], in_=big[:, 18:20, :].rearrange("p a t -> p (a t)")), reads=["C_f0", "C_f1"], dma=dsem)
            P.op("act", lambda e: e.activation(out=rsg[:], in_=ssqg[:], func=AF.Sqrt, bias=EPS, scale=1.0 / 512), reads=["ssqg"], writes=["rsg"])
            P.op("dve", lambda e: e.reciprocal(out=rsg[:], in_=rsg[:]), reads=["rsg"], writes=["rsg"])
            for g in range(2):
                gs = slice(g * 512, (g + 1) * 512)
                o_nw = PBC["ssdnw"][0]
                P.op("dve", (lambda g, gs: lambda e: e.scalar_tensor_tensor(out=yn[:, gs], in0=ytok[:, gs], scalar=rsg[:, g:g + 1],
                                                                             in1=pbc[:, o_nw + gs.start:o_nw + gs.stop], op0=ALU.mult, op1=ALU.mult))(g, gs),
                     reads=["ytok%d" % g, "rsg", "pbc"], writes=["yn%d" % g])
            for half in range(2):
                def tr_y(e, half=half):
                    ins = None
                    for q in range(4):
                        fb = half * 4 + q
                        ins = e.transpose(out=pH_bf[:, q * 128:(q + 1) * 128], in_=yn[:, fb * 128:(fb + 1) * 128], identity=ident[:])
                    return ins
                P.op("pe", tr_y, reads=["yn%d" % half, "ident"], writes=["pHt"])
                P.op("act", (lambda half, tsl: lambda e: e.activation(out=yT[:, half * 4:(half + 1) * 4, tsl],
                                                                        in_=pH_bf.rearrange("p (q t) -> p q t", q=4), func=AF.Identity))(half, tsl),
                     reads=["pHt"], writes=["yT%d" % fb for fb in range(half * 4, half * 4 + 4)])

        trip = {}
        sc_pending = []
        for m in range(24):
            j, which = m // 3, m % 3
            n = 5 + m // 4
            ws, wkey = wslot(ti, n)
            w3 = ws.rearrange("p (k n) -> p k n", k=8)
            bank, bkey = next_acc()

            def mm(e, w3=w3, bank=bank, q=m % 4):
                ins = None
                for k in range(8):
                    ins = e.matmul(bank[:], lhsT=w3[:, k, q * 128:(q + 1) * 128], rhs=hT[:, k, :], start=(k == 0), stop=(k == 7))
                return ins
            P.op("pe", mm, reads=hT_keys + [wkey], writes=[bkey])
            trip[which] = (bank, bkey)
            if which == 2 and sc_pending:
                sc_pending.pop(0)()
            if which == 1:
                P.op("act", (lambda bank: lambda e: e.activation(out=csb[:], in_=bank[:], func=AF.Identity))(bank), reads=[bkey], writes=["csb"])
            if which == 0:
                P.op("act", (lambda bank: lambda e: e.activation(out=tmpb[:, 0, :], in_=bank[:], func=AF.Identity))(bank), reads=[bkey], writes=["tmpb0"])
            if which == 2:
                (bh, kh) = trip[2]
                wb_ = cwork[:, j % 2, :]
                wk = "cwork%d" % (j % 2)
                acc = cacc[:, 2 + j % 2, :]
                ak = "cacc%d" % (2 + j % 2)
                hk = "halo_sc%d" % j
                cw = lambda k, j=j: pfm[:, PFM["sc_cw"][0] + j * 3 + k:PFM["sc_cw"][0] + j * 3 + k + 1]
                P.op("dve", (lambda wb_, bh: lambda e: e.tensor_tensor(out=wb_[:, 2:514], in0=bh[:], in1=csb[:], op=ALU.mult))(wb_, bh),
                     reads=[kh, "csb"], writes=[wk])
                P.op("pool", (lambda wb_, j: lambda e: e.tensor_copy(out=wb_[:, 0:2], in_=halo_sc[:, j, :]))(wb_, j), reads=[hk], writes=[wk])
                P.op("pool", (lambda wb_, j: lambda e: e.tensor_copy(out=halo_sc[:, j, :], in_=wb_[:, 512:514]))(wb_, j), reads=[wk], writes=[hk])
                P.op("act", (lambda acc, wb_, cw: lambda e: e.activation(out=acc, in_=wb_[:, 0:512], func=AF.Identity, scale=cw(0)))(acc, wb_, cw),
                     reads=[wk, "pfm"], writes=[ak])
                for k in (1, 2):
                    P.op("dve", (lambda acc, wb_, cw, k: lambda e: e.scalar_tensor_tensor(out=acc, in0=wb_[:, k:k + 512], scalar=cw(k), in1=acc,
                                                                                          op0=ALU.mult, op1=ALU.add))(acc, wb_, cw, k),
                         reads=[wk, ak, "pfm"], writes=[ak])
                P.op("dve", (lambda acc: lambda e: e.tensor_tensor(out=acc, in0=acc, in1=tmpb[:, 0, :], op=ALU.mult))(acc), reads=[ak, "tmpb0"], writes=[ak])
                P.op("act", (lambda acc: lambda e: e.activation(out=sqb[:], in_=acc, func=AF.Square))(acc), reads=[ak], writes=["sqb"])

                def tail(acc=acc, ak=ak, j=j):
                    P.op("pe", lambda e: e.matmul(pA[:], lhsT=G64[:], rhs=sqb[:], start=True, stop=True), reads=["sqb", "G64"], writes=["pA"])
                    P.op("act", lambda e: e.activation(out=rtb[:], in_=pA[:], func=AF.Sqrt, bias=EPS, scale=1.0), reads=["pA"], writes=["rtb"])
                    P.op("dve", lambda e: e.reciprocal(out=rtb[:], in_=rtb[:]), reads=["rtb"], writes=["rtb"])
                    nw = pfm[:, PFM["scnw"][0] + j:PFM["scnw"][0] + j + 1]
                    P.op("dve", lambda e: e.scalar_tensor_tensor(out=yT[:, 8 + j, :], in0=acc, scalar=nw, in1=rtb[:], op0=ALU.mult, op1=ALU.mult),
                         reads=[ak, "rtb", "pfm"], writes=["yT%d" % (8 + j)])
                sc_pending.append(tail)
        while sc_pending:
            sc_pending.pop(0)()

        yT_keys = ["yT%d" % fb for fb in range(16)]

        def proj_res(first_n, nchunks, nk_last, src_t, src_keys, gbc, gkey):
            slots = [wslot(ti, first_n + q) for q in range(nchunks)]
            for tb in range(NTB):
                for half in range(2):
                    bank, bkey = next_acc()

                    def mm(e, bank=bank, tb=tb, half=half):
                        ins = None
                        nk = (nchunks - 1) * 4 + nk_last
                        for kk in range(nk):
                            q, r = kk // 4, kk % 4
                            w3 = slots[q][0].rearrange("p (r d) -> p r d", r=4)
                            ins = e.matmul(bank[:], lhsT=src_t[:, kk, tb * 128:(tb + 1) * 128], rhs=w3[:, r, half * 512:(half + 1) * 512],
                                           start=(kk == 0), stop=(kk == nk - 1))
                        return ins
                    P.op("pe", mm, reads=src_keys + [s[1] for s in slots], writes=[bkey])
                    hs = slice(half * 512, (half + 1) * 512)
                    P.op("dve", (lambda bank, half, hs: lambda e: e.tensor_tensor(out=tmpb[:, half, :], in0=bank[:], in1=gbc[:, hs], op=ALU.mult))(bank, half, hs),
                         reads=[bkey, gkey], writes=["tmpb%d" % half])
                    P.op("pool", (lambda tb, half, hs: lambda e: e.tensor_tensor(out=xt[:, tb, hs], in0=xt[:, tb, hs], in1=tmpb[:, half, :], op=ALU.add))(tb, half, hs),
                         reads=["tmpb%d" % half, "xt%d" % tb], writes=["xt%d" % tb])

        proj_res(11, 4, 4, yT, yT_keys, g1bc, "g1bc")


        rmsnorm_to_hT(s2, sh2, "s2")
        rdp, wrp = (ti + 1) % 2, ti % 2
        fcw = pfm[:, PFM["ffn_cw"][0]:PFM["ffn_cw"][0] + 132].rearrange("p (b k) -> p b k", k=3)
        Hrd = halo_ffn[:, rdp]
        rk, wk_h = "halo_ffn_p%d" % rdp, "halo_ffn_p%d" % wrp
        P.op("pool", lambda e, Hrd=Hrd: e.tensor_tensor(out=corr[:, :, 0], in0=Hrd[:, :, 0], in1=fcw[:, :, 0], op=ALU.mult), reads=[rk, "pfm"], writes=["corr"])
        P.op("pool", lambda e, Hrd=Hrd: e.tensor_tensor(out=tmpc[:], in0=Hrd[:, :, 1], in1=fcw[:, :, 1], op=ALU.mult), reads=[rk, "pfm"], writes=["tmpc"])
        P.op("pool", lambda e: e.tensor_tensor(out=corr[:, :, 0], in0=corr[:, :, 0], in1=tmpc[:], op=ALU.add), reads=["corr", "tmpc"], writes=["corr"])
        P.op("pool", lambda e, Hrd=Hrd: e.tensor_tensor(out=corr[:, :, 1], in0=Hrd[:, :, 1], in1=fcw[:, :, 0], op=ALU.mult), reads=[rk, "pfm", "corr"], writes=["corr"])
        for c in range(11):
            ws, wkey = wslot(ti, 15 + c)
            w3 = ws.rearrange("p (k n) -> p k n", k=8)
            for q in range(4):
                isv = q >= 2
                jb = 2 * c + (q % 2)
                blk = (22 if isv else 0) + jb
                bank, bkey = next_acc()

                def mm(e, w3=w3, bank=bank, q=q):
                    ins = None
                    for k in range(8):
                        ins = e.matmul(bank[:], lhsT=w3[:, k, q * 128:(q + 1) * 128], rhs=hT[:, k, :], start=(k == 0), stop=(k == 7))
                    return ins
                P.op("pe", mm, reads=hT_keys + [wkey], writes=[bkey])
                acc = cacc[:, q, :]
                ak = "cacc%d" % q
                cw = lambda k, blk=blk: pfm[:, PFM["ffn_cw"][0] + blk * 3 + k:PFM["ffn_cw"][0] + blk * 3 + k + 1]
                cb = pfm[:, PFM["ffn_cb"][0] + blk:PFM["ffn_cb"][0] + blk + 1]
                P.op("act", (lambda acc, bank, cw, cb: lambda e: e.activation(out=acc, in_=bank[:], func=AF.Identity, bias=cb, scale=cw(2)))(acc, bank, cw, cb),
                     reads=[bkey, "pfm"], writes=[ak])
                P.op("act", (lambda bank, blk, wrp: lambda e: e.activation(out=halo_ffn[:, wrp, blk, :], in_=bank[:, 510:512], func=AF.Identity))(bank, blk, wrp),
                     reads=[bkey], writes=[wk_h + "_%d" % blk])
                P.op("dve", (lambda acc, bank, cw: lambda e: e.scalar_tensor_tensor(out=acc[:, 1:512], in0=bank[:, 0:511], scalar=cw(1), in1=acc[:, 1:512],
                                                                                    op0=ALU.mult, op1=ALU.add))(acc, bank, cw), reads=[bkey, ak, "pfm"], writes=[ak])
                P.op("dve", (lambda acc, bank, cw: lambda e: e.scalar_tensor_tensor(out=acc[:, 2:512], in0=bank[:, 0:510], scalar=cw(0), in1=acc[:, 2:512],
                                                                                    op0=ALU.mult, op1=ALU.add))(acc, bank, cw), reads=[bkey, ak, "pfm"], writes=[ak])
                P.op("pool", (lambda acc, blk: lambda e: e.tensor_tensor(out=acc[:, 0:2], in0=acc[:, 0:2], in1=corr[:, blk, :], op=ALU.add))(acc, blk),
                     reads=[ak, "corr"], writes=[ak])
                if not isv:
                    P.op("act", (lambda acc, q: lambda e: e.activation(out=sgb[:, q, :], in_=acc, func=AF.Silu))(acc, q), reads=[ak], writes=["sgb%d" % q])
                else:
                    P.op("pool", (lambda acc, q, jb: lambda e: e.tensor_tensor(out=big[:, jb, :], in0=sgb[:, q - 2, :], in1=acc, op=ALU.mult))(acc, q, jb),
                         reads=[ak, "sgb%d" % (q - 2)], writes=["act%d" % jb, "big"])
        P.op("act", lambda e: e.activation(out=tmpc[:, 0:1], in_=tmpc[:, 0:1], func=AF.Identity),
             reads=[wk_h + "_%d" % b for b in range(44)] + ["tmpc"], writes=[wk_h, "tmpc"])
        act_keys = ["act%d" % j for j in range(22)]
        proj_res(26, 6, 2, big, act_keys, g2bc, "g2bc")

        for tb in range(NTB):
            P.op("dve", (lambda tb: lambda e: e.scalar_tensor_tensor(out=junk[:], in0=xt[:, tb, :], scalar=1.0, in1=xt[:, tb, :],
                                                                      op0=ALU.mult, op1=ALU.mult, accum_out=ssq[:, 4 + tb:5 + tb]))(tb),
                 reads=["xt%d" % tb], writes=["junk", "ssq"])
        P.op("act", lambda e: e.activation(out=rstd[:, 4:8], in_=ssq[:, 4:8], func=AF.Sqrt, bias=EPS, scale=1.0 / D), reads=["ssq"], writes=["rstd"])
        P.op("dve", lambda e: e.reciprocal(out=rstd[:, 4:8], in_=rstd[:, 4:8]), reads=["rstd"], writes=["rstd"])
        for tb in range(NTB):
            P.op("dve", (lambda tb: lambda e: e.scalar_tensor_tensor(out=xt[:, tb, :], in0=xt[:, tb, :], scalar=rstd[:, 4 + tb:5 + tb], in1=pb("fnw"),
                                                                      op0=ALU.mult, op1=ALU.mult))(tb),
                 reads=["xt%d" % tb, "rstd", "pbc"], writes=["xt%d" % tb])
        P.op("sp", (lambda ti: lambda e: e.dma_start(out=out_d[ti * T:(ti + 1) * T, :].rearrange("(tb p) d -> p tb d", p=128), in_=xt[:]))(ti),
             reads=xt_keys, writes=["outd"], dma=o_sem)

    P.op("sp", lambda e: e.nop(), reads=["outd"], writes=["fin"])
    if dbg:
        P.op("sp", lambda e: e.nop(), reads=["state0", "ytok0"], writes=["fin2"])

    with nc.Block() as block:
        P.emit(block)
    es.close()
    return nc, P


_CACHE = {}


def _get_prog(L, dbg=False):
    key = (L, dbg)
    if key not in _CACHE:
        _CACHE[key] = build_program(L, dbg)
    return _CACHE[key]


def make_in_maps(inputs, ncores):
    maps = []
    f = lambda a: np.ascontiguousarray(np.asarray(a, np.float32))
    w_ada, w_in, w_out, w_up, w_dn = f(inputs["w_ada"][0]), f(inputs["w_in"][0]), f(inputs["w_out"][0]), f(inputs["w_up"][0]), f(inputs["w_down"][0])
    for b in range(ncores):
        pfm, pbc, pbg = pack_params(inputs, b)
        maps.append({"x": f(inputs["x"][b]), "pfm": pfm, "pbc": pbc, "pbg": pbg, "w_ada": w_ada, "w_in": w_in, "w_out": w_out, "w_up": w_up, "w_down": w_dn})
    return maps


def kernel(**inputs):
    x = np.asarray(inputs["x"])
    B, L, _ = x.shape
    nc, _ = _get_prog(L)
    maps = make_in_maps(inputs, B)
    res = run_bass_kernel_spmd(nc, maps, core_ids=list(range(B)))
    return np.stack([np.asarray(r["out"], np.float32) for r in res.results], axis=0)
```
